# Optimizing a Trainium2 kernel written in Bass

```python
import math
import jax, jax.numpy as jnp
from jax import lax
import numpy as np

D_MODEL = 1024
BATCH = 32
SEQ = 2048
DEPTH = 2

N_A_LAYERS = DEPTH // 2
N_B_LAYERS = DEPTH - N_A_LAYERS
DN_ALPHA = (2 * DEPTH) ** 0.25
DN_BETA = (8 * DEPTH) ** -0.25
SSD_EXPAND = 2
SSD_D_INNER = SSD_EXPAND * D_MODEL
SSD_HEAD_DIM = 64
SSD_N_HEADS = SSD_D_INNER // SSD_HEAD_DIM
SSD_N_GROUPS = 8
SSD_HEADS_PER_GROUP = SSD_N_HEADS // SSD_N_GROUPS
SSD_D_STATE = 128
SSD_CONV = 4
SSD_CHUNK = 128
SSD_GN = SSD_N_GROUPS * SSD_D_STATE
SSD_CONV_DIM = SSD_D_INNER + 2 * SSD_GN
SSD_IN_DIM = 2 * SSD_D_INNER + 2 * SSD_GN + SSD_N_HEADS
SSD_DT_MIN = 0.001
SSD_DT_MAX = 0.1
MLA_N_HEADS = D_MODEL // 128
MLA_Q_RANK = 384
MLA_KV_RANK = 256
MLA_NOPE = 128
MLA_ROPE = 64
MLA_V = 128
ROPE_THETA = 10000.0
Q_BLOCK = 128
MAX_POS_OFFSET = 1024
FFN_HIDDEN = 2816
FFN_CONV = 3
LN_EPS = 1e-5
RMS_EPS = 1e-6

kernel_name = 'yoco_ssd_mla_convffn_deepnorm'


def layer_norm(x, g, b):
    xf = x.astype(jnp.float32)
    mu = jnp.mean(xf, axis=-1, keepdims=True)
    var = jnp.mean(jnp.square(xf - mu), axis=-1, keepdims=True)
    return ((xf - mu) * lax.rsqrt(var + LN_EPS) * g.astype(jnp.float32) + b.astype(jnp.float32)).astype(x.dtype)


def rms_norm(x, g, eps=RMS_EPS):
    xf = x.astype(jnp.float32)
    y = xf * lax.rsqrt(jnp.mean(xf * xf, axis=-1, keepdims=True) + eps)
    return (y * g.astype(jnp.float32)).astype(x.dtype)


def causal_depthwise_conv(x, w, b):
    width, ch = w.shape
    y = lax.conv_general_dilated(x, w[:, None, :].astype(x.dtype), window_strides=(1,),
                                 padding=[(width - 1, 0)],
                                 dimension_numbers=('NWC', 'WIO', 'NWC'),
                                 feature_group_count=ch)
    return y + b.astype(x.dtype)


def rope_tables(positions):
    inv_freq = 1.0 / (ROPE_THETA ** (jnp.arange(0, MLA_ROPE, 2, dtype=jnp.float32) / MLA_ROPE))
    ang = positions.astype(jnp.float32)[..., None] * inv_freq
    return jnp.cos(ang), jnp.sin(ang)


def apply_rope(x, cos, sin):
    xf = x.astype(jnp.float32)
    x1, x2 = jnp.split(xf, 2, axis=-1)
    return jnp.concatenate([x1 * cos - x2 * sin, x2 * cos + x1 * sin], axis=-1).astype(x.dtype)


def ssd_chunked_scan(xs, dt, A, Bm, Cm):
    b, s = xs.shape[:2]
    nc = s // SSD_CHUNK

    def chunks(a):
        return jnp.moveaxis(a.reshape((b, nc, SSD_CHUNK) + a.shape[2:]), 1, 0)

    causal = jnp.tril(jnp.ones((SSD_CHUNK, SSD_CHUNK), dtype=bool))[None, :, :, None, None]

    def step(state, inp):
        xc, dtc, bc, cc = inp
        cum = jnp.cumsum(dtc * A, axis=1)
        seg = cum[:, :, None] - cum[:, None, :]
        decay = jnp.exp(jnp.where(causal, seg, -jnp.inf))
        cb = jnp.einsum('btgn,bsgn->btsg', cc, bc)
        w = cb[..., None] * decay * dtc[:, None]
        y = jnp.einsum('btsgk,bsgkp->btgkp', w, xc)
        y = y + jnp.einsum('btgn,bgkpn->btgkp', cc, state) * jnp.exp(cum)[..., None]
        w_end = jnp.exp(cum[:, -1:] - cum) * dtc
        state = (state * jnp.exp(cum[:, -1])[..., None, None]
                 + jnp.einsum('bsgk,bsgkp,bsgn->bgkpn', w_end, xc, bc))
        return state, y

    init = jnp.zeros((b, SSD_N_GROUPS, SSD_HEADS_PER_GROUP, SSD_HEAD_DIM, SSD_D_STATE), jnp.float32)
    _, y = lax.scan(step, init, (chunks(xs), chunks(dt), chunks(Bm), chunks(Cm)))
    return jnp.moveaxis(y, 0, 1).reshape(xs.shape)


def ssd_mixer(x, in_proj, conv_w, conv_b, dt_bias, A_log, D, norm_g, out_proj):
    b, s, _ = x.shape
    G, K, P, N = SSD_N_GROUPS, SSD_HEADS_PER_GROUP, SSD_HEAD_DIM, SSD_D_STATE
    zxbcdt = x @ in_proj
    z = zxbcdt[..., :SSD_D_INNER]
    xbc = zxbcdt[..., SSD_D_INNER:SSD_D_INNER + SSD_CONV_DIM]
    dt = zxbcdt[..., SSD_D_INNER + SSD_CONV_DIM:]
    xbc = jax.nn.silu(causal_depthwise_conv(xbc, conv_w, conv_b))
    xs = xbc[..., :SSD_D_INNER].reshape(b, s, G, K, P).astype(jnp.float32)
    Bm = xbc[..., SSD_D_INNER:SSD_D_INNER + SSD_GN].reshape(b, s, G, N).astype(jnp.float32)
    Cm = xbc[..., SSD_D_INNER + SSD_GN:].reshape(b, s, G, N).astype(jnp.float32)
    dt = jax.nn.softplus(dt.astype(jnp.float32) + dt_bias.astype(jnp.float32)).reshape(b, s, G, K)
    A = -jnp.exp(A_log.astype(jnp.float32)).reshape(G, K)
    y = ssd_chunked_scan(xs, dt, A, Bm, Cm)
    y = y + D.astype(jnp.float32).reshape(G, K)[..., None] * xs
    y = y.reshape(b, s, SSD_D_INNER) * jax.nn.silu(z.astype(jnp.float32))
    yg = y.reshape(b, s, G, SSD_D_INNER // G)
    yg = yg * lax.rsqrt(jnp.mean(yg * yg, axis=-1, keepdims=True) + LN_EPS)
    y = (yg.reshape(b, s, SSD_D_INNER) * norm_g.astype(jnp.float32)).astype(x.dtype)
    return y @ out_proj


def mla_shared_kv(h, kv_down_proj, kv_norm_g, kv_up_k, kv_up_v, cos, sin):
    b, s, _ = h.shape
    ckv_kr = h @ kv_down_proj
    c_kv = rms_norm(ckv_kr[..., :MLA_KV_RANK], kv_norm_g)
    k_rope = apply_rope(ckv_kr[..., MLA_KV_RANK:], cos, sin)
    k_nope = (c_kv @ kv_up_k).reshape(b, s, MLA_N_HEADS, MLA_NOPE)
    v = (c_kv @ kv_up_v).reshape(b, s, MLA_N_HEADS, MLA_V)
    return k_nope, k_rope, v


def mla_attention(h, q_down, q_norm_g, q_up, out_proj, k_nope, k_rope, v, cos, sin):
    b, s, _ = h.shape
    c_q = rms_norm(h @ q_down, q_norm_g)
    q = (c_q @ q_up).reshape(b, s, MLA_N_HEADS, MLA_NOPE + MLA_ROPE)
    q_nope = q[..., :MLA_NOPE]
    q_rope = apply_rope(q[..., MLA_NOPE:], cos[:, :, None], sin[:, :, None])
    nb = s // Q_BLOCK
    scale = (MLA_NOPE + MLA_ROPE) ** -0.5
    key_idx = jnp.arange(s)

    def to_blocks(a):
        return jnp.moveaxis(a.reshape((b, nb, Q_BLOCK) + a.shape[2:]), 1, 0)

    def attend_block(args):
        qn, qr, blk = args
        scores = (jnp.einsum('bqhd,bkhd->bhqk', qn, k_nope)
                  + jnp.einsum('bqhr,bkr->bhqk', qr, k_rope)).astype(jnp.float32) * scale
        q_idx = blk * Q_BLOCK + jnp.arange(Q_BLOCK)
        scores = jnp.where(key_idx[None, :] <= q_idx[:, None], scores, -jnp.inf)
        p = jax.nn.softmax(scores, axis=-1).astype(v.dtype)
        return jnp.einsum('bhqk,bkhd->bqhd', p, v)

    o = lax.map(attend_block, (to_blocks(q_nope), to_blocks(q_rope), jnp.arange(nb)))
    o = jnp.moveaxis(o, 0, 1).reshape(b, s, MLA_N_HEADS * MLA_V)
    return o @ out_proj


def conv_ffn(h, up, conv_w, conv_b, down):
    u = causal_depthwise_conv(h @ up, conv_w, conv_b)
    g, val = jnp.split(u, 2, axis=-1)
    return (jax.nn.silu(g) * val) @ down


def setup_inputs(seed: int = 0) -> dict:
    key = jax.random.key(seed)
    ks = jax.random.split(key, 32)
    f32 = jnp.float32

    def nrm(k, shape, scale):
        return jax.random.normal(k, shape, f32) * scale

    na, nbl, d = N_A_LAYERS, N_B_LAYERS, D_MODEL
    x = nrm(ks[0], (BATCH, SEQ, d), 1.0)
    offset = jax.random.randint(ks[1], (BATCH, 1), 0, MAX_POS_OFFSET, dtype=jnp.int32)
    positions = (offset + jnp.arange(SEQ, dtype=jnp.int32)[None, :]).astype(jnp.int32)

    u = jax.random.uniform(ks[2], (na, SSD_N_HEADS), f32)
    dt0 = jnp.exp(u * (math.log(SSD_DT_MAX) - math.log(SSD_DT_MIN)) + math.log(SSD_DT_MIN))
    dt0 = jnp.maximum(dt0, 1e-4)
    ssd_dt_bias = dt0 + jnp.log(-jnp.expm1(-dt0))
    ssd_A_log = jnp.log(jax.random.uniform(ks[3], (na, SSD_N_HEADS), f32, 1.0, 16.0))

    return {
        'x': x,
        'positions': positions,
        'ssd_in_proj': nrm(ks[4], (na, d, SSD_IN_DIM), d ** -0.5),
        'ssd_conv_w': nrm(ks[5], (na, SSD_CONV, SSD_CONV_DIM), SSD_CONV ** -0.5),
        'ssd_conv_b': nrm(ks[6], (na, SSD_CONV_DIM), 0.02),
        'ssd_dt_bias': ssd_dt_bias,
        'ssd_A_log': ssd_A_log,
        'ssd_D': 1.0 + nrm(ks[7], (na, SSD_N_HEADS), 0.1),
        'ssd_norm_g': 1.0 + nrm(ks[8], (na, SSD_D_INNER), 0.02),
        'ssd_out_proj': nrm(ks[9], (na, SSD_D_INNER, d), DN_BETA * SSD_D_INNER ** -0.5),
        'kv_down_proj': nrm(ks[10], (d, MLA_KV_RANK + MLA_ROPE), d ** -0.5),
        'kv_norm_g': 1.0 + nrm(ks[11], (MLA_KV_RANK,), 0.02),
        'kv_up_k': nrm(ks[12], (MLA_KV_RANK, MLA_N_HEADS * MLA_NOPE), MLA_KV_RANK ** -0.5),
        'kv_up_v': nrm(ks[13], (MLA_KV_RANK, MLA_N_HEADS * MLA_V), DN_BETA * MLA_KV_RANK ** -0.5),
        'q_down_proj': nrm(ks[14], (nbl, d, MLA_Q_RANK), d ** -0.5),
        'q_norm_g': 1.0 + nrm(ks[15], (nbl, MLA_Q_RANK), 0.02),
        'q_up_proj': nrm(ks[16], (nbl, MLA_Q_RANK, MLA_N_HEADS * (MLA_NOPE + MLA_ROPE)), MLA_Q_RANK ** -0.5),
        'attn_out_proj': nrm(ks[17], (nbl, MLA_N_HEADS * MLA_V, d), DN_BETA * (MLA_N_HEADS * MLA_V) ** -0.5),
        'ffn_up': nrm(ks[18], (DEPTH, d, 2 * FFN_HIDDEN), d ** -0.5),
        'ffn_conv_w': nrm(ks[19], (DEPTH, FFN_CONV, 2 * FFN_HIDDEN), FFN_CONV ** -0.5),
        'ffn_conv_b': nrm(ks[20], (DEPTH, 2 * FFN_HIDDEN), 0.02),
        'ffn_down': nrm(ks[21], (DEPTH, FFN_HIDDEN, d), DN_BETA * FFN_HIDDEN ** -0.5),
        'ln_mix_g': 1.0 + nrm(ks[22], (DEPTH, d), 0.02),
        'ln_mix_b': nrm(ks[23], (DEPTH, d), 0.02),
        'ln_ffn_g': 1.0 + nrm(ks[24], (DEPTH, d), 0.02),
        'ln_ffn_b': nrm(ks[25], (DEPTH, d), 0.02),
    }


def reference(x, positions, ssd_in_proj, ssd_conv_w, ssd_conv_b, ssd_dt_bias, ssd_A_log, ssd_D,
              ssd_norm_g, ssd_out_proj, kv_down_proj, kv_norm_g, kv_up_k, kv_up_v, q_down_proj,
              q_norm_g, q_up_proj, attn_out_proj, ffn_up, ffn_conv_w, ffn_conv_b, ffn_down,
              ln_mix_g, ln_mix_b, ln_ffn_g, ln_ffn_b):
    cos, sin = rope_tables(positions)
    h = x
    shared_kv = None
    for i in range(DEPTH):
        if i < N_A_LAYERS:
            mix = ssd_mixer(h, ssd_in_proj[i], ssd_conv_w[i], ssd_conv_b[i], ssd_dt_bias[i],
                            ssd_A_log[i], ssd_D[i], ssd_norm_g[i], ssd_out_proj[i])
        else:
            j = i - N_A_LAYERS
            k_nope, k_rope, v = shared_kv
            mix = mla_attention(h, q_down_proj[j], q_norm_g[j], q_up_proj[j], attn_out_proj[j],
                                k_nope, k_rope, v, cos, sin)
        h = layer_norm(DN_ALPHA * h + mix, ln_mix_g[i], ln_mix_b[i])
        ff = conv_ffn(h, ffn_up[i], ffn_conv_w[i], ffn_conv_b[i], ffn_down[i])
        h = layer_norm(DN_ALPHA * h + ff, ln_ffn_g[i], ln_ffn_b[i])
        if i == N_A_LAYERS - 1:
            shared_kv = mla_shared_kv(h, kv_down_proj, kv_norm_g, kv_up_k, kv_up_v, cos, sin)
    return h
```

```python
import math
import types
from contextlib import ExitStack

import numpy as np
import concourse.bass as bass
import concourse.mybir as mybir
from concourse.bass_utils import run_bass_kernel_spmd

F32 = mybir.dt.float32
BF16 = mybir.dt.bfloat16
I32 = mybir.dt.int32
AF = mybir.ActivationFunctionType
ALU = mybir.AluOpType

PE, DVE, ACT, POOL, SP = "tensor", "vector", "scalar", "gpsimd", "sync"
ENGINES = [PE, DVE, ACT, POOL, SP]

D = 1024
T = 512
NCH = 4
DI = 2048
NH = 32
NG = 8
NST = 128
FH = 2816
NFC = 22
KVR = 256
QR = 384
AH = 8
ALPHA = float((2 * 2) ** 0.25)
LN_EPS = 1e-5
RMS_EPS = 1e-6
SCALE = float(192 ** -0.5)
BIGM = 30000.0
TWO_PI = 2.0 * math.pi
CW1 = 6.28125
CW2 = TWO_PI - CW1


class Buf:
    __slots__ = ("name", "writer", "readers")

    def __init__(self, name=""):
        self.name = name
        self.writer = None
        self.readers = {}


class Op:
    __slots__ = ("eng", "fn", "deps", "is_dma", "sig", "sem", "val", "idx")

    def __init__(self, eng, fn, is_dma):
        self.eng = eng
        self.fn = fn
        self.deps = {}
        self.is_dma = is_dma
        self.sig = False
        self.sem = None
        self.val = 0
        self.idx = 0


def _freeze(fn):
    if fn.__closure__ is None:
        return fn
    cells = []
    for c in fn.__closure__:
        try:
            cells.append(types.CellType(c.cell_contents))
        except ValueError:
            cells.append(c)
    return types.FunctionType(fn.__code__, fn.__globals__, fn.__name__, fn.__defaults__, tuple(cells))


class Prog:
    SEM_EPOCH = 60000

    def __init__(self, nc):
        self.nc = nc
        self.ops = []
        self.last = {}
        self.barrier_ops = None
        self.barrier_seen = set()

    def _dep(self, o, d):
        if d is None or d is o:
            return
        key = ("dma", d.idx) if d.is_dma else d.eng
        cur = o.deps.get(key)
        if cur is None or cur.idx < d.idx:
            o.deps[key] = d

    def op(self, eng, fn, reads=(), writes=(), is_dma=False):
        o = Op(eng, _freeze(fn), is_dma)
        o.idx = len(self.ops)
        for b in reads:
            self._dep(o, b.writer)
        for b in writes:
            self._dep(o, b.writer)
            for r in b.readers.values():
                self._dep(o, r)
        if self.barrier_ops is not None and eng not in self.barrier_seen:
            self.barrier_seen.add(eng)
            for d in self.barrier_ops:
                self._dep(o, d)
        for b in reads:
            key = ("dma", o.idx) if is_dma else eng
            b.readers[key] = o
        for b in writes:
            b.writer = o
            b.readers = {}
        self.ops.append(o)
        if is_dma:
            self.last.setdefault("dmas", []).append(o)
        else:
            self.last[eng] = o
        return o

    def dma(self, eng, out, in_, reads=(), writes=(), **kw):
        return self.op(eng, lambda e: e.dma_start(out=out, in_=in_, **kw), reads, writes, is_dma=True)

    def barrier(self):
        ops = [v for k, v in self.last.items() if k != "dmas"]
        ops += self.last.get("dmas", [])
        self.last["dmas"] = []
        self.barrier_ops = ops
        self.barrier_seen = set()

    def emit(self):
        nc = self.nc
        ops = self.ops
        for o in ops:
            if o.is_dma:
                o.sig = True
        for o in ops:
            for d in o.deps.values():
                if d.is_dma:
                    continue
                if d.eng == o.eng and not o.is_dma and d.eng == PE:
                    continue
                d.sig = True
        with ExitStack() as st:
            cnt = {e: 0 for e in ENGINES}
            cur_sem = {e: None for e in ENGINES}
            dma_sems = {e: [] for e in ENGINES}
            dma_cnt = {e: [] for e in ENGINES}
            drr = {e: 0 for e in ENGINES}
            NDS = 8
            nsem = [0]

            def new_sem(name):
                nsem[0] += 1
                return st.enter_context(nc.semaphore("%s_%d" % (name, nsem[0])))

            for o in ops:
                e = o.eng
                if o.is_dma:
                    if not dma_sems[e]:
                        dma_sems[e] = [new_sem("d" + e) for _ in range(NDS)]
                        dma_cnt[e] = [0] * NDS
                    k = drr[e]
                    drr[e] = (k + 1) % NDS
                    if dma_cnt[e][k] + 16 > self.SEM_EPOCH:
                        dma_sems[e][k] = new_sem("d" + e)
                        dma_cnt[e][k] = 0
                    dma_cnt[e][k] += 16
                    o.sem = dma_sems[e][k]
                    o.val = dma_cnt[e][k]
                elif o.sig:
                    if cur_sem[e] is None or cnt[e] >= self.SEM_EPOCH:
                        cur_sem[e] = new_sem("s" + e)
                        cnt[e] = 0
                    cnt[e] += 1
                    o.sem = cur_sem[e]
                    o.val = cnt[e]
            per = {e: [o for o in ops if o.eng == e] for e in ENGINES}
            self.stats = {e: len(per[e]) for e in ENGINES}
            self.nsem = nsem[0]
            block = st.enter_context(nc.Block())

            def run(engname, eng):
                waited = {}
                for o in per[engname]:
                    need = {}
                    for d in o.deps.values():
                        if not d.sig:
                            continue
                        if (not d.is_dma) and d.eng == engname and not o.is_dma and engname == PE:
                            continue
                        key = id(d.sem)
                        if key not in need or need[key][1] < d.val:
                            need[key] = (d.sem, d.val)
                    if o.is_dma and o.val > 16:
                        key = id(o.sem)
                        pv = o.val - 16
                        if key not in need or need[key][1] < pv:
                            need[key] = (o.sem, pv)
                    for key, (sem, val) in need.items():
                        if waited.get(key, 0) >= val:
                            continue
                        eng.wait_ge(sem, val)
                        waited[key] = val
                    inst = o.fn(eng)
                    if o.sig:
                        inst.then_inc(o.sem, 16 if o.is_dma else 1)

            if per[PE]:
                @block.tensor
                def _(eng):
                    run(PE, eng)
            if per[DVE]:
                @block.vector
                def _(eng):
                    run(DVE, eng)
            if per[ACT]:
                @block.scalar
                def _(eng):
                    run(ACT, eng)
            if per[POOL]:
                @block.gpsimd
                def _(eng):
                    run(POOL, eng)
            if per[SP]:
                @block.sync
                def _(eng):
                    run(SP, eng)


def _kslab(w):
    k, n = w.shape
    kc = k // 128
    return np.ascontiguousarray(w.reshape(kc, 128, n).transpose(1, 0, 2).reshape(128, kc * n))


def _slab_defs(inp):
    sl = {}
    w_in = inp["ssd_in_proj"][0]
    sl["in_dt"] = _kslab(w_in[:, 6144:6176])
    for g in range(NG):
        cols = np.concatenate([
            2048 + g * 256 + np.arange(256),
            2048 + 2048 + g * 128 + np.arange(128),
            2048 + 3072 + g * 128 + np.arange(128)])
        sl["in_xbc%d" % g] = _kslab(w_in[:, cols])
        sl["in_z%d" % g] = _kslab(w_in[:, g * 256:(g + 1) * 256])
    w_out = inp["ssd_out_proj"][0]
    for half in range(2):
        for ch in range(2):
            sl["out%d%d" % (half, ch)] = _kslab(w_out[ch * 1024:(ch + 1) * 1024, half * 512:(half + 1) * 512])

    def ffn(l):
        up = inp["ffn_up"][l]
        dn = inp["ffn_down"][l]
        for s in range(11):
            cols = np.concatenate([
                (2 * s) * 128 + np.arange(128), (2 * s + 1) * 128 + np.arange(128),
                FH + (2 * s) * 128 + np.arange(128), FH + (2 * s + 1) * 128 + np.arange(128)])
            sl["up%d_%d" % (l, s)] = _kslab(up[:, cols])
        for half in range(2):
            for part, (a, b) in enumerate(((0, 8), (8, 16), (16, 22))):
                sl["dn%d_%d_%d" % (l, half, part)] = _kslab(dn[a * 128:b * 128, half * 512:(half + 1) * 512])

    ffn(0)
    sl["kvd"] = _kslab(inp["kv_down_proj"])
    sl["kuk"] = _kslab(inp["kv_up_k"])
    sl["kuv"] = _kslab(inp["kv_up_v"])
    sl["qd"] = _kslab(inp["q_down_proj"][0])
    qu = inp["q_up_proj"][0]
    ncols = np.concatenate([h * 192 + np.arange(128) for h in range(AH)])
    rcols = np.concatenate([h * 192 + 128 + np.arange(64) for h in range(AH)])
    scols = np.concatenate([h * 192 + 128 + np.concatenate([32 + np.arange(32), np.arange(32)]) for h in range(AH)])
    sl["qun"] = _kslab(qu[:, ncols])
    sl["qur"] = _kslab(qu[:, rcols])
    sl["qus"] = _kslab(qu[:, scols])
    wo = inp["attn_out_proj"][0]
    for half in range(2):
        sl["ao%d" % half] = _kslab(wo[:, half * 512:(half + 1) * 512])
    ffn(1)
    return sl


def _slab_shapes():
    shp = {"in_dt": 8 * 32}
    for g in range(NG):
        shp["in_xbc%d" % g] = 8 * 512
        shp["in_z%d" % g] = 8 * 256
    for half in range(2):
        for ch in range(2):
            shp["out%d%d" % (half, ch)] = 8 * 512

    def ffn(l):
        for s in range(11):
            shp["up%d_%d" % (l, s)] = 8 * 512
        for half in range(2):
            for part, n in enumerate((8, 8, 6)):
                shp["dn%d_%d_%d" % (l, half, part)] = n * 512

    ffn(0)
    shp["kvd"] = 8 * 320
    shp["kuk"] = 2 * 1024
    shp["kuv"] = 2 * 1024
    shp["qd"] = 8 * 384
    shp["qun"] = 3 * 1024
    shp["qur"] = 3 * 512
    shp["qus"] = 3 * 512
    for half in range(2):
        shp["ao%d" % half] = 8 * 512
    ffn(1)
    return shp


SLAB_N = _slab_shapes()
SLAB_OFF = {}
_o = 0
for _k, _n in SLAB_N.items():
    SLAB_OFF[_k] = _o
    _o += _n
TOT = _o
WSLOT = 4096

_c = 0


def _cadd(n):
    global _c
    o = _c
    _c += n
    return o


C_SCW = _cadd(32 * 4)
C_SCB = _cadd(32)
C_FCW = [_cadd(44 * 3), _cadd(44 * 3)]
C_FCB = [_cadd(44), _cadd(44)]
C_NG = _cadd(16)
C_DTB = _cadd(32)
C_ALOG = _cadd(32)
C_DD = _cadd(32)
C_KVG = _cadd(2)
C_QG = _cadd(3)
C_INVF = _cadd(64)
C_INVF2 = _cadd(1)
C_SGN = _cadd(1)
C_PH = _cadd(64)
C_ID = _cadd(128)
C_TRIGT = _cadd(128)
C_BIGI = _cadd(128)
C_MLE = _cadd(128)
C_NEGM = _cadd(128)
C_ONES = _cadd(128)
NCST = _c


def _pack_consts(inp):
    c = np.zeros((128, NCST), np.float32)
    p = np.arange(128)
    cw = inp["ssd_conv_w"][0]
    cb = inp["ssd_conv_b"][0]
    for g in range(NG):
        for j in range(4):
            if j < 2:
                ch = g * 256 + j * 128 + p
            elif j == 2:
                ch = 2048 + g * 128 + p
            else:
                ch = 3072 + g * 128 + p
            idx = g * 4 + j
            c[:, C_SCW + idx * 4:C_SCW + idx * 4 + 4] = cw[:, ch].T
            c[:, C_SCB + idx] = cb[ch]
    for l in range(2):
        fw = inp["ffn_conv_w"][l]
        fb = inp["ffn_conv_b"][l]
        for s in range(11):
            for q in range(4):
                j = 2 * s + (q % 2)
                ch = (FH if q >= 2 else 0) + j * 128 + p
                idx = s * 4 + q
                c[:, C_FCW[l] + idx * 3:C_FCW[l] + idx * 3 + 3] = fw[:, ch].T
                c[:, C_FCB[l] + idx] = fb[ch]
    c[:, C_NG:C_NG + 16] = inp["ssd_norm_g"][0].reshape(16, 128).T
    c[:, C_DTB:C_DTB + 32] = inp["ssd_dt_bias"][0][None, :]
    c[:, C_ALOG:C_ALOG + 32] = inp["ssd_A_log"][0][None, :]
    c[:, C_DD:C_DD + 32] = inp["ssd_D"][0][None, :]
    c[:, C_KVG:C_KVG + 2] = inp["kv_norm_g"].reshape(2, 128).T
    c[:, C_QG:C_QG + 3] = inp["q_norm_g"][0].reshape(3, 128).T
    invf = (1.0 / (10000.0 ** (np.arange(0, 64, 2, dtype=np.float32) / np.float32(64)))).astype(np.float32)
    c[:, C_INVF:C_INVF + 32] = invf[None, :]
    c[:, C_INVF + 32:C_INVF + 64] = invf[None, :]
    c[:, C_INVF2] = invf[p % 32]
    c[:, C_SGN] = np.where((p % 64) < 32, -1.0, 1.0)
    c[:, C_PH:C_PH + 32] = 0.0
    c[:, C_PH + 32:C_PH + 64] = math.pi / 2
    k = p[:, None]
    t = p[None, :]
    c[:, C_ID:C_ID + 128] = (k == t)
    c[:, C_TRIGT:C_TRIGT + 128] = (k > t)
    c[:, C_BIGI:C_BIGI + 128] = (k == t) * BIGM
    c[:, C_MLE:C_MLE + 128] = (k <= t)
    c[:, C_NEGM:C_NEGM + 128] = np.where(k > t, -BIGM, 0.0)
    c[:, C_ONES:C_ONES + 128] = 1.0
    return c


class BD(dict):
    def __missing__(self, k):
        self[k] = Buf(k)
        return self[k]


def build_program(nseq, S, debug=None):
    nc = bass.Bass("TRN2", target_bir_lowering=False)
    NT = S // T
    NB = S // 128
    x_d = nc.dram_tensor("x", [nseq, S, D], F32, kind="ExternalInput").ap()
    pos_d = nc.dram_tensor("pos", [nseq, S], I32, kind="ExternalInput").ap()
    wall_d = nc.dram_tensor("wall", [128, TOT], F32, kind="ExternalInput").ap()
    cst_d = nc.dram_tensor("cst", [128, NCST], F32, kind="ExternalInput").ap()
    lnp_d = nc.dram_tensor("lnp", [8, D], F32, kind="ExternalInput").ap()
    out_d = nc.dram_tensor("out", [nseq, S, D], F32, kind="ExternalOutput").ap()
    wbf_d = nc.dram_tensor("wbf", [128, TOT], BF16, kind="Internal").ap()
    dbg_d = {}
    if debug:
        for name, shape in debug.items():
            if name.startswith("_"):
                continue
            dbg_d[name] = nc.dram_tensor("dbg_" + name, shape, F32, kind="ExternalOutput").ap()

    st = ExitStack()
    with st:
        def sb(name, shape, dt):
            return st.enter_context(nc.sbuf_tensor("s_" + name, shape, dt))

        P = Prog(nc)
        B = BD()

        cst = sb("cst", [128, NCST], F32)
        identb = sb("identb", [128, 128], BF16)
        onesb = sb("onesb", [128, 128], BF16)
        negmb = sb("negmb", [128, 128], BF16)
        abc = sb("abc", [128, 32], F32)
        KnT = sb("KnT", [128, AH, S], BF16)
        KrT = sb("KrT", [64, S], BF16)
        Vt = sb("Vt", [128, NB, D], BF16)
        stf = sb("stf", [128, DI], F32)
        stb = sb("stb", [128, DI], BF16)
        halo_s = sb("halo_s", [128, 32, 3], F32)
        halo_f = [sb("halo_f%d" % l, [128, 44, 2], F32) for l in range(2)]
        wbuf = [sb("wbuf%d" % i, [128, WSLOT], BF16) for i in range(3)]
        hres = sb("hres", [128, NCH, D], F32)
        hT = sb("hT", [128, 8, T], BF16)
        lnt = sb("lnt", [128, 2, D], F32)
        ARENA = 30 * 1024
        arena = sb("arena", [128, ARENA], BF16)
        fin = sb("fin", [128, 8], F32)
        posi_t = sb("posi_t", [128, NCH], I32)
        posT_i = sb("posT_i", [64, T], I32)
        PS = [st.enter_context(nc.psum_tensor("ps%d" % i, [128, 512], F32)) for i in range(8)]
        PSB = [p_[:].bitcast(BF16) for p_ in PS]

        ar_off = [0]

        def ar_reset():
            ar_off[0] = 0

        def ar(n_elems, dt):
            mult = 2 if dt in (F32, I32) else 1
            a = ar_off[0]
            ar_off[0] += n_elems * mult
            assert ar_off[0] <= ARENA, ("arena overflow", ar_off[0])
            v = arena[:, a:a + n_elems * mult]
            if dt != BF16:
                v = v.bitcast(dt)
            return v

        def dbg(name, ap, buf):
            if debug and name in dbg_d:
                P.dma(SP, dbg_d[name], ap, reads=[buf], writes=[B["dbg_" + name]])

        P.dma(SP, cst[:], cst_d, writes=[B["cst"]])
        Bc = B["cst"]
        P.op(DVE, lambda e: e.tensor_copy(identb[:], cst[:, C_ID:C_ID + 128]), [Bc], [B["identb"]])
        P.op(DVE, lambda e: e.tensor_copy(onesb[:], cst[:, C_ONES:C_ONES + 128]), [Bc], [B["onesb"]])
        P.op(DVE, lambda e: e.tensor_copy(negmb[:], cst[:, C_NEGM:C_NEGM + 128]), [Bc], [B["negmb"]])
        P.op(ACT, lambda e: e.activation(out=abc[:], in_=cst[:, C_ALOG:C_ALOG + 32], func=AF.Exp), [Bc], [B["abc"]])
        P.op(DVE, lambda e: e.tensor_scalar(out=abc[:], in0=abc[:], scalar1=-1.0, scalar2=None, op0=ALU.mult),
             [B["abc"]], [B["abc"]])
        ar_reset()
        stg32 = [ar(WSLOT, F32) for _ in range(2)]
        stg16 = [ar(WSLOT, BF16) for _ in range(2)]
        cast_engs = [DVE, ACT, POOL]
        slab_items = list(SLAB_N.items())

        def pro_load(i):
            name, n = slab_items[i]
            off = SLAB_OFF[name]
            k = i % 2
            P.dma(SP, stg32[k][:, 0:n], wall_d[:, off:off + n], writes=[B["stg32_%d" % k]])

        pro_load(0)
        for i, (name, n) in enumerate(slab_items):
            off = SLAB_OFF[name]
            k = i % 2
            s32, s16 = stg32[k], stg16[k]
            b32, b16 = B["stg32_%d" % k], B["stg16_%d" % k]
            for q0 in range(0, n, 2048):
                q1 = min(n, q0 + 2048)
                P.op(DVE, lambda e, s32=s32, s16=s16, q0=q0, q1=q1: e.tensor_copy(s16[:, q0:q1], s32[:, q0:q1]), [b32], [b16])
            if i + 1 < len(slab_items):
                pro_load(i + 1)
            P.dma(SP, wbf_d[:, off:off + n], s16[:, 0:n], reads=[b16], writes=[B["wbf"]])
        P.barrier()

        wstate = {"i": 0}

        def wload(name):
            i = wstate["i"]
            wstate["i"] += 1
            slot = wbuf[i % 3]
            bslot = B["wslot%d" % (i % 3)]
            n = SLAB_N[name]
            off = SLAB_OFF[name]
            P.dma(SP, slot[:, 0:n], wbf_d[:, off:off + n], reads=[B["wbf"]], writes=[bslot])
            return slot[:, 0:n], bslot

        def layer_norm_tile(li, hb):
            P.dma(SP, lnt[:, 0, :], lnp_d[2 * li:2 * li + 1, :].partition_broadcast(128), writes=[B["lnt0"]])
            P.dma(SP, lnt[:, 1, :], lnp_d[2 * li + 1:2 * li + 2, :].partition_broadcast(128), writes=[B["lnt1"]])
            stats = ar(NCH * 12, F32)
            mv = ar(NCH * 2, F32)
            rstd = ar(NCH, F32)
            for c in range(NCH):
                bh = B["hres%d" % c]
                bs = B["lnstat%d" % c]
                for hf in range(2):
                    P.op(DVE, lambda e, c=c, hf=hf: e.bn_stats(out=stats[:, c * 12 + hf * 6:c * 12 + hf * 6 + 6],
                                                               in_=hres[:, c, hf * 512:(hf + 1) * 512]), [bh], [bs])
                P.op(DVE, lambda e, c=c: e.bn_aggr(out=mv[:, 2 * c:2 * c + 2], in_=stats[:, c * 12:c * 12 + 12]), [bs], [bs])
            bs_all = [B["lnstat%d" % c] for c in range(NCH)]
            mv3 = mv.rearrange("p (c two) -> p c two", two=2)
            P.op(DVE, lambda e: e.tensor_scalar(out=rstd, in0=mv3[:, :, 1], scalar1=LN_EPS, scalar2=None, op0=ALU.add),
                 bs_all, [B["lnrstd"]])
            P.op(ACT, lambda e: e.activation(out=rstd, in_=rstd, func=AF.Ln), [B["lnrstd"]], [B["lnrstd"]])
            P.op(ACT, lambda e: e.activation(out=rstd, in_=rstd, func=AF.Exp, scale=-0.5), [B["lnrstd"]], [B["lnrstd"]])
            for c in range(NCH):
                bh = B["hres%d" % c]
                P.op(DVE, lambda e, c=c: e.tensor_scalar(out=hres[:, c, :], in0=hres[:, c, :], scalar1=mv[:, 2 * c:2 * c + 1],
                                                         scalar2=rstd[:, c:c + 1], op0=ALU.subtract, op1=ALU.mult),
                     [bh, B["lnrstd"], B["lnstat%d" % c]], [bh])
                P.op(POOL, lambda e, c=c: e.tensor_tensor(out=hres[:, c, :], in0=hres[:, c, :], in1=lnt[:, 0, :], op=ALU.mult),
                     [bh, B["lnt0"]], [bh])
                P.op(POOL, lambda e, c=c: e.tensor_tensor(out=hres[:, c, :], in0=hres[:, c, :], in1=lnt[:, 1, :], op=ALU.add),
                     [bh, B["lnt1"]], [bh])
                P.op(ACT, lambda e, c=c: e.activation(out=hb[:, c, :], in_=hres[:, c, :], func=AF.Copy), [bh], [B["hb%d" % c]])
            transpose_h(hb)

        def transpose_h(hb):
            for c in range(NCH):
                pb = PSB[2 + (c % 2)]
                bp = B["ps%d" % (2 + (c % 2))]
                for kc in range(8):
                    P.op(PE, lambda e, c=c, kc=kc, pb=pb: e.transpose(pb[:, kc * 128:(kc + 1) * 128],
                                                                    hb[:, c, kc * 128:(kc + 1) * 128], identb[:]),
                         [B["hb%d" % c], B["identb"]], [bp])
                eng = ACT if c % 2 == 0 else DVE
                src = pb[:, 0:1024].rearrange("p (k t) -> p k t", k=8)
                dst = hT[:, :, c * 128:(c + 1) * 128]
                if eng == ACT:
                    P.op(ACT, lambda e, src=src, dst=dst: e.activation(out=dst, in_=src, func=AF.Copy), [bp], [B["hT"]])
                else:
                    P.op(DVE, lambda e, src=src, dst=dst: e.tensor_copy(dst, src), [bp], [B["hT"]])

        def rstd_from_ss(ss, n, eps, bufs):
            P.op(DVE, lambda e: e.tensor_scalar(out=ss, in0=ss, scalar1=1.0 / n, scalar2=eps, op0=ALU.mult, op1=ALU.add),
                 bufs, bufs)
            P.op(ACT, lambda e: e.activation(out=ss, in_=ss, func=AF.Ln), bufs, bufs)
            P.op(ACT, lambda e: e.activation(out=ss, in_=ss, func=AF.Exp, scale=-0.5), bufs, bufs)

        def ssd_phase(seq, ti):
            ar_reset()
            t0 = ti * T
            xb = ar(NCH * D, BF16).rearrange("p (c d) -> p c d", c=NCH)
            for c in range(NCH):
                P.dma(SP, hres[:, c, :], x_d[seq, t0 + c * 128:t0 + (c + 1) * 128, :], writes=[B["hres%d" % c]])
                P.op(ACT if c % 2 else DVE,
                     (lambda e, c=c: e.activation(out=xb[:, c, :], in_=hres[:, c, :], func=AF.Copy)) if c % 2 else
                     (lambda e, c=c: e.tensor_copy(xb[:, c, :], hres[:, c, :])),
                     [B["hres%d" % c]], [B["hb%d" % c]])
            transpose_h(xb)
            dtr = ar(128, F32)
            dte = ar(128, F32)
            dt = ar(128, F32)
            aa = ar(128, F32)
            cum = ar(128, F32)
            nb = ar(128, F32)
            ecum = ar(128, F32)
            wend = ar(128, F32)
            datot = ar(128, F32)
            wdt, bw = wload("in_dt")
            wdt3 = wdt.rearrange("p (k n) -> p k n", k=8)
            for c in range(NCH):
                for kc in range(8):
                    P.op(PE, lambda e, c=c, kc=kc: e.matmul(PS[7][:, c * 32:(c + 1) * 32], lhsT=hT[:, kc, c * 128:(c + 1) * 128],
                                                            rhs=wdt3[:, kc, :], start=(kc == 0), stop=(kc == 7)),
                         [B["hT"], bw], [B["ps7"]])
            dtb = cst[:, C_DTB:C_DTB + 32]
            Bd = B["dtq"]
            v3 = lambda a: a.rearrange("p (c h) -> p c h", c=NCH)
            bc32 = lambda a: a.unsqueeze(1).to_broadcast([128, NCH, 32])
            P.op(DVE, lambda e: e.tensor_tensor(out=v3(dtr), in0=v3(PS[7][:, 0:128]), in1=bc32(dtb), op=ALU.add),
                 [B["ps7"], Bc], [Bd])
            P.op(DVE, lambda e: e.tensor_scalar(out=dte, in0=dtr, scalar1=30.0, scalar2=None, op0=ALU.min), [Bd], [Bd])
            P.op(ACT, lambda e: e.activation(out=dte, in_=dte, func=AF.Exp), [Bd], [Bd])
            P.op(ACT, lambda e: e.activation(out=dte, in_=dte, func=AF.Ln, bias=cst[:, C_ONES:C_ONES + 1], scale=1.0), [Bd, Bc], [Bd])
            P.op(DVE, lambda e: e.tensor_tensor(out=dt, in0=dte, in1=dtr, op=ALU.max), [Bd], [Bd])
            P.op(DVE, lambda e: e.tensor_tensor(out=v3(aa), in0=v3(dt), in1=bc32(abc[:]), op=ALU.mult), [Bd, B["abc"]], [Bd])
            P.op(PE, lambda e: e.matmul(PS[7][:, 128:256], lhsT=cst[:, C_MLE:C_MLE + 128], rhs=aa, start=True, stop=True),
                 [Bd, Bc], [B["ps7b"]])
            P.op(PE, lambda e: e.matmul(PS[7][:, 256:384], lhsT=cst[:, C_ONES:C_ONES + 128], rhs=aa, start=True, stop=True),
                 [Bd, Bc], [B["ps7c"]])
            P.op(DVE, lambda e: e.tensor_copy(cum, PS[7][:, 128:256]), [B["ps7b"]], [Bd])
            P.op(DVE, lambda e: e.tensor_tensor(out=nb, in0=PS[7][:, 256:384], in1=cum, op=ALU.subtract), [B["ps7c"], Bd], [Bd])
            P.op(ACT, lambda e: e.activation(out=ecum, in_=cum, func=AF.Exp), [Bd], [Bd])
            P.op(ACT, lambda e: e.activation(out=wend, in_=nb, func=AF.Exp), [Bd], [Bd])
            P.op(ACT, lambda e: e.activation(out=datot, in_=PS[7][:, 256:384], func=AF.Exp), [B["ps7c"]], [Bd])
            P.op(DVE, lambda e: e.tensor_tensor(out=wend, in0=wend, in1=dt, op=ALU.mult), [Bd], [Bd])
            if ti == 0 and seq == 0:
                dbg("dt", dt, Bd); dbg("cum", cum, Bd); dbg("nb", nb, Bd); dbg("dtr", dtr, Bd)

            ynT = ar(16 * T, BF16).rearrange("p (k t) -> p k t", k=16)
            fT = ar(4 * T, BF16).rearrange("p (j t) -> p j t", j=4)
            zs = ar(NCH * 256, F32).rearrange("p (c n) -> p c n", c=NCH)
            stg = [ar(T + 3, F32) for _ in range(2)]
            acc = [ar(T, F32) for _ in range(2)]
            tok = ar(NCH * 384, BF16).rearrange("p (c n) -> p c n", c=NCH)
            cbm = ar(128, F32)
            Ls = [ar(128, F32) for _ in range(2)]
            dec = [ar(128, F32) for _ in range(2)]
            WT = [ar(128, BF16) for _ in range(4)]
            yi = ar(256, F32)
            yy = ar(256, F32)
            ytmp = ar(256, F32)
            yn = ar(256, BF16)
            xw = ar(256, BF16)
            ss1 = ar(1, F32)
            junk = ar(256, F32)

            for g in range(NG):
                wx, bwx = wload("in_xbc%d" % g)
                wx3 = wx.rearrange("p (k n) -> p k n", k=8)
                wz, bwz = wload("in_z%d" % g)
                wz3 = wz.rearrange("p (k n) -> p k n", k=8)
                if ti == 0 and seq == 0 and g == 0:
                    P.op(DVE, lambda e: e.tensor_copy(junk, wz[:, 0:256]), [bwz], [B["junk"]])
                    dbg("wz", junk, B["junk"])
                    P.op(DVE, lambda e: e.tensor_copy(yi, wx[:, 0:256]), [bwx], [B["yy"]])
                    dbg("wx", yi, B["yy"])
                for j in range(4):
                    pj = j % 2
                    bp = B["ps%d" % pj]
                    for kc in range(8):
                        P.op(PE, lambda e, j=j, kc=kc, pj=pj: e.matmul(PS[pj][:, :], lhsT=wx3[:, kc, j * 128:(j + 1) * 128],
                                                                       rhs=hT[:, kc, :], start=(kc == 0), stop=(kc == 7)),
                             [B["hT"], bwx], [bp])
                    ci = g * 4 + j
                    sg = stg[pj]
                    ac = acc[pj]
                    bsg, bac = B["stg%d" % pj], B["acc%d" % pj]
                    bhalo = B["halo_s"]
                    cw = lambda k, ci=ci: cst[:, C_SCW + ci * 4 + k:C_SCW + ci * 4 + k + 1]
                    cbv = cst[:, C_SCB + ci:C_SCB + ci + 1]
                    P.op(ACT, lambda e, sg=sg, pj=pj: e.activation(out=sg[:, 3:T + 3], in_=PS[pj][:, :], func=AF.Copy), [bp], [bsg])
                    P.op(POOL, lambda e, sg=sg, ci=ci: e.tensor_copy(sg[:, 0:3], halo_s[:, ci, :]), [bhalo], [bsg])
                    P.op(ACT, lambda e, ac=ac, pj=pj, cw=cw, cbv=cbv: e.activation(out=ac, in_=PS[pj][:, :], func=AF.Identity,
                                                                                 scale=cw(3), bias=cbv), [bp, Bc], [bac])
                    for k in range(3):
                        P.op(DVE, lambda e, ac=ac, sg=sg, k=k, cw=cw: e.scalar_tensor_tensor(
                            out=ac, in0=sg[:, k:k + T], scalar=cw(k), in1=ac, op0=ALU.mult, op1=ALU.add), [bsg, bac, Bc], [bac])
                    P.op(POOL, lambda e, sg=sg, ci=ci: e.tensor_copy(halo_s[:, ci, :], sg[:, T:T + 3]), [bsg], [bhalo])
                    if ti == 0 and seq == 0 and g == 0 and j == 0:
                        dbg("acc00", ac, bac)
                    P.op(ACT, lambda e, ac=ac, j=j: e.activation(out=fT[:, j, :], in_=ac, func=AF.Silu), [bac], [B["fT%d" % j]])
                for c in range(NCH):
                    pz = 2 + (c // 2)
                    zc = (c % 2) * 256
                    bpz = B["ps%d" % pz]
                    for kc in range(8):
                        P.op(PE, lambda e, c=c, kc=kc, pz=pz, zc=zc: e.matmul(PS[pz][:, zc:zc + 256], lhsT=hT[:, kc, c * 128:(c + 1) * 128],
                                                                              rhs=wz3[:, kc, :], start=(kc == 0), stop=(kc == 7)),
                             [B["hT"], bwz], [bpz])
                    P.op(ACT, lambda e, c=c, pz=pz, zc=zc: e.activation(out=zs[:, c, :], in_=PS[pz][:, zc:zc + 256], func=AF.Silu),
                         [bpz], [B["zs%d" % c]])
                for c in range(NCH):
                    pt = 4 + (c % 2)
                    bpt = B["ps%dt" % pt]
                    for j in range(3):
                        P.op(PE, lambda e, c=c, j=j, pt=pt: e.transpose(PSB[pt][:, j * 128:(j + 1) * 128], fT[:, j, c * 128:(c + 1) * 128],
                                                                        identb[:]), [B["fT%d" % j], B["identb"]], [bpt])
                    P.op(DVE, lambda e, c=c, pt=pt: e.tensor_copy(tok[:, c, :], PSB[pt][:, 0:384]), [bpt], [B["tok%d" % c]])
                for c in range(NCH):
                    cs = slice(c * 128, (c + 1) * 128)
                    btok = B["tok%d" % c]
                    P.op(PE, lambda e, cs=cs: e.matmul(PS[6][:, 0:128], lhsT=fT[:, 2, cs], rhs=fT[:, 3, cs], start=True, stop=True),
                         [B["fT2"], B["fT3"]], [B["ps6a"]])
                    P.op(DVE, lambda e: e.tensor_tensor(out=cbm, in0=PS[6][:, 0:128], in1=cst[:, C_MLE:C_MLE + 128], op=ALU.mult),
                         [B["ps6a"], Bc], [B["cbm"]])
                    for hh in range(4):
                        h = g * 4 + hh
                        col = c * 32 + h
                        L = Ls[hh % 2]
                        dc = dec[hh % 2]
                        bL, bdc = B["L%d" % (hh % 2)], B["dec%d" % (hh % 2)]
                        bseg = B["seg%d" % hh]
                        bwt = B["WT%d" % hh]
                        P.op(POOL, lambda e, L=L, col=col: e.tensor_scalar(out=L, in0=cst[:, C_BIGI:C_BIGI + 128], scalar1=aa[:, col:col + 1],
                                                                         scalar2=None, op0=ALU.add), [Bc, Bd], [bL])
                        P.op(PE, lambda e, L=L, hh=hh: e.matmul(PS[7][:, hh * 128:(hh + 1) * 128], lhsT=L, rhs=cst[:, C_TRIGT:C_TRIGT + 128],
                                                                start=True, stop=True), [bL, Bc], [bseg])
                        P.op(ACT, lambda e, dc=dc, hh=hh, col=col: e.activation(out=dc, in_=PS[7][:, hh * 128:(hh + 1) * 128], func=AF.Exp,
                                                                                scale=-1.0, bias=nb[:, col:col + 1]), [bseg, Bd], [bdc])
                        P.op(DVE, lambda e, dc=dc, hh=hh, col=col: e.scalar_tensor_tensor(out=WT[hh], in0=dc, scalar=dt[:, col:col + 1], in1=cbm,
                                                                                          op0=ALU.mult, op1=ALU.mult), [bdc, Bd, B["cbm"]], [bwt])
                        P.op(PE, lambda e, hh=hh, c=c: e.matmul(PS[6][:, 128 + hh * 64:128 + (hh + 1) * 64], lhsT=WT[hh],
                                                                rhs=tok[:, c, hh * 64:(hh + 1) * 64], start=True, stop=True),
                             [bwt, btok], [B["ps6y"]])
                    P.op(PE, lambda e, cs=cs, g=g: e.matmul(PS[0][:, 0:256], lhsT=fT[:, 3, cs], rhs=stb[:, g * 256:(g + 1) * 256],
                                                            start=True, stop=True), [B["fT3"], B["stb%d" % g]], [B["ps0"]])
                    ec = ecum.rearrange("p (c h) -> p c h", c=NCH)[:, c, g * 4:(g + 1) * 4]
                    v4 = lambda a: a.rearrange("p (h q) -> p h q", h=4)
                    bc4 = lambda a: a.unsqueeze(2).to_broadcast([128, 4, 64])
                    By = B["yy"]
                    P.op(DVE, lambda e, ec=ec: e.tensor_tensor(out=v4(yi), in0=v4(PS[0][:, 0:256]), in1=bc4(ec), op=ALU.mult),
                         [B["ps0"], Bd], [By])
                    P.op(DVE, lambda e: e.tensor_tensor(out=yy, in0=PS[6][:, 128:384], in1=yi, op=ALU.add), [B["ps6y"], By], [By])
                    dd = cst[:, C_DD + g * 4:C_DD + (g + 1) * 4]
                    P.op(POOL, lambda e, c=c, dd=dd: e.tensor_tensor(out=v4(ytmp), in0=v4(tok[:, c, 0:256]), in1=bc4(dd), op=ALU.mult),
                         [btok, Bc], [B["ytmp"]])
                    P.op(DVE, lambda e: e.tensor_tensor(out=yy, in0=yy, in1=ytmp, op=ALU.add), [By, B["ytmp"]], [By])
                    P.op(DVE, lambda e, c=c: e.tensor_tensor(out=yy, in0=yy, in1=zs[:, c, :], op=ALU.mult), [By, B["zs%d" % c]], [By])
                    if ti == 0 and seq == 0 and g == 0 and c == 0:
                        dbg("yy00", yy, By); dbg("cbm", cbm, B["cbm"]); dbg("dec", dec[1], B["dec1"]); dbg("zs0", zs[:, 0, :], B["zs0"])
                    P.op(ACT, lambda e: e.activation(out=junk, in_=yy, func=AF.Square, accum_out=ss1), [By], [B["ss1"], B["junk"]])
                    rstd_from_ss(ss1, 256.0, LN_EPS, [B["ss1"]])
                    P.op(DVE, lambda e: e.tensor_scalar(out=yn, in0=yy, scalar1=ss1[:, 0:1], scalar2=None, op0=ALU.mult),
                         [By, B["ss1"]], [B["yn"]])
                    for j in range(2):
                        P.op(PE, lambda e, j=j: e.transpose(PSB[5][:, 512 + j * 128:512 + (j + 1) * 128], yn[:, j * 128:(j + 1) * 128], identb[:]),
                             [B["yn"], B["identb"]], [B["ps5b"]])
                        kk = 2 * g + j
                        P.op(ACT, lambda e, j=j, kk=kk, cs=cs: e.activation(out=ynT[:, kk, cs], in_=PSB[5][:, 512 + j * 128:512 + (j + 1) * 128],
                                                                            func=AF.Copy, scale=cst[:, C_NG + kk:C_NG + kk + 1]),
                             [B["ps5b"], Bc], [B["ynT"]])
                    we = wend.rearrange("p (c h) -> p c h", c=NCH)[:, c, g * 4:(g + 1) * 4]
                    da = datot.rearrange("p (c h) -> p c h", c=NCH)[:, c, g * 4:(g + 1) * 4]
                    P.op(POOL, lambda e, c=c, we=we: e.tensor_tensor(out=v4(xw), in0=v4(tok[:, c, 0:256]), in1=bc4(we), op=ALU.mult),
                         [btok, Bd], [B["xw"]])
                    P.op(PE, lambda e, c=c: e.matmul(PS[4][:, 256:512], lhsT=tok[:, c, 256:384], rhs=xw, start=True, stop=True),
                         [btok, B["xw"]], [B["ps4s"]])
                    stg_ = stf[:, g * 256:(g + 1) * 256]
                    bst = B["stf%d" % g]
                    P.op(DVE, lambda e, stg_=stg_, da=da: e.tensor_tensor(out=v4(stg_), in0=v4(stg_), in1=bc4(da), op=ALU.mult), [bst, Bd], [bst])
                    P.op(DVE, lambda e, stg_=stg_: e.tensor_tensor(out=stg_, in0=stg_, in1=PS[4][:, 256:512], op=ALU.add), [bst, B["ps4s"]], [bst])
                    P.op(ACT, lambda e, stg_=stg_, g=g: e.activation(out=stb[:, g * 256:(g + 1) * 256], in_=stg_, func=AF.Copy),
                         [bst], [B["stb%d" % g]])
            for half in range(2):
                for ch in range(2):
                    wo, bwo = wload("out%d%d" % (half, ch))
                    wo3 = wo.rearrange("p (k n) -> p k n", k=8)
                    for c in range(NCH):
                        for i in range(8):
                            P.op(PE, lambda e, c=c, i=i, ch=ch, wo3=wo3: e.matmul(PS[c][:, :], lhsT=ynT[:, ch * 8 + i, c * 128:(c + 1) * 128],
                                                                                  rhs=wo3[:, i, :], start=(ch == 0 and i == 0),
                                                                                  stop=(ch == 1 and i == 7)),
                                 [B["ynT"], bwo], [B["ps%d" % c]])
                for c in range(NCH):
                    hs = hres[:, c, half * 512:(half + 1) * 512]
                    P.op(DVE, lambda e, c=c, hs=hs: e.scalar_tensor_tensor(out=hs, in0=hs, scalar=ALPHA, in1=PS[c][:, :], op0=ALU.mult, op1=ALU.add),
                         [B["hres%d" % c], B["ps%d" % c]], [B["hres%d" % c]])
            if ti == 0 and seq == 0:
                dbg("res", hres[:, 0, :], B["hres0"])
            layer_norm_tile(0, xb)

        def ffn_phase(l):
            ar_reset()
            hbf = ar(NCH * D, BF16).rearrange("p (c d) -> p c d", c=NCH)
            act = ar(NFC * T, BF16).rearrange("p (k t) -> p k t", k=NFC)
            stg = [ar(T + 2, F32) for _ in range(4)]
            acc = [ar(T, F32) for _ in range(4)]
            sg_ = [ar(T, F32) for _ in range(2)]
            for s in range(11):
                wu, bwu = wload("up%d_%d" % (l, s))
                wu3 = wu.rearrange("p (k n) -> p k n", k=8)
                for q in range(4):
                    pq = 4 + q
                    bp = B["ps%d" % pq]
                    for kc in range(8):
                        P.op(PE, lambda e, q=q, kc=kc, pq=pq, wu3=wu3: e.matmul(PS[pq][:, :], lhsT=wu3[:, kc, q * 128:(q + 1) * 128],
                                                                                rhs=hT[:, kc, :], start=(kc == 0), stop=(kc == 7)),
                             [B["hT"], bwu], [bp])
                    ci = s * 4 + q
                    sg, ac = stg[q], acc[q]
                    bsg, bac = B["fstg%d" % q], B["facc%d" % q]
                    bhalo = B["halo_f%d" % l]
                    cw = lambda k, ci=ci: cst[:, C_FCW[l] + ci * 3 + k:C_FCW[l] + ci * 3 + k + 1]
                    cbv = cst[:, C_FCB[l] + ci:C_FCB[l] + ci + 1]
                    hl = halo_f[l]
                    P.op(ACT, lambda e, sg=sg, pq=pq: e.activation(out=sg[:, 2:T + 2], in_=PS[pq][:, :], func=AF.Copy), [bp], [bsg])
                    P.op(POOL, lambda e, sg=sg, ci=ci, hl=hl: e.tensor_copy(sg[:, 0:2], hl[:, ci, :]), [bhalo], [bsg])
                    P.op(ACT, lambda e, ac=ac, pq=pq, cw=cw, cbv=cbv: e.activation(out=ac, in_=PS[pq][:, :], func=AF.Identity,
                                                                                 scale=cw(2), bias=cbv), [bp, Bc], [bac])
                    for k in range(2):
                        P.op(DVE, lambda e, ac=ac, sg=sg, k=k, cw=cw: e.scalar_tensor_tensor(
                            out=ac, in0=sg[:, k:k + T], scalar=cw(k), in1=ac, op0=ALU.mult, op1=ALU.add), [bsg, bac, Bc], [bac])
                    P.op(POOL, lambda e, sg=sg, ci=ci, hl=hl: e.tensor_copy(hl[:, ci, :], sg[:, T:T + 2]), [bsg], [bhalo])
                for q in range(2):
                    j = 2 * s + q
                    P.op(ACT, lambda e, q=q: e.activation(out=sg_[q], in_=acc[q], func=AF.Silu), [B["facc%d" % q]], [B["fsg%d" % q]])
                    P.op(POOL, lambda e, q=q, j=j: e.tensor_tensor(out=act[:, j, :], in0=sg_[q], in1=acc[2 + q], op=ALU.mult),
                         [B["fsg%d" % q], B["facc%d" % (2 + q)]], [B["act"]])
            for half in range(2):
                for part, (a, b_) in enumerate(((0, 8), (8, 16), (16, 22))):
                    wd, bwd = wload("dn%d_%d_%d" % (l, half, part))
                    wd3 = wd.rearrange("p (k n) -> p k n", k=b_ - a)
                    for c in range(NCH):
                        for i in range(b_ - a):
                            P.op(PE, lambda e, c=c, i=i, a=a, wd3=wd3: e.matmul(PS[c][:, :], lhsT=act[:, a + i, c * 128:(c + 1) * 128],
                                                                                rhs=wd3[:, i, :], start=(a + i == 0), stop=(a + i == NFC - 1)),
                                 [B["act"], bwd], [B["ps%d" % c]])
                for c in range(NCH):
                    hs = hres[:, c, half * 512:(half + 1) * 512]
                    P.op(DVE, lambda e, c=c, hs=hs: e.scalar_tensor_tensor(out=hs, in0=hs, scalar=ALPHA, in1=PS[c][:, :], op0=ALU.mult, op1=ALU.add),
                         [B["hres%d" % c], B["ps%d" % c]], [B["hres%d" % c]])
            layer_norm_tile(2 * l + 1, hbf)

        def rope_tables(seq, ti):
            t0 = ti * T
            posi = ar(NCH, I32)
            posf = ar(NCH, F32)
            ang = ar(NCH * 64, F32)
            kf = ar(NCH * 64, F32)
            ki = ar(NCH * 64, I32)
            sc = ang
            Br = B["rope_t"]
            for c in range(NCH):
                P.dma(SP, posi_t[:, c:c + 1], pos_d[seq, t0 + c * 128:t0 + (c + 1) * 128].rearrange("(p o) -> p o", o=1), writes=[Br])
            P.op(DVE, lambda e: e.tensor_copy(posf, posi_t[:]), [Br], [Br])
            a3 = lambda a: a.rearrange("p (c i) -> p c i", c=NCH)
            P.op(DVE, lambda e: e.tensor_tensor(out=a3(ang), in0=posf.unsqueeze(2).to_broadcast([128, NCH, 64]),
                                                in1=cst[:, C_INVF:C_INVF + 64].unsqueeze(1).to_broadcast([128, NCH, 64]), op=ALU.mult),
                 [Br, Bc], [Br])
            P.op(DVE, lambda e: e.tensor_tensor(out=a3(ang), in0=a3(ang),
                                                in1=cst[:, C_PH:C_PH + 64].unsqueeze(1).to_broadcast([128, NCH, 64]), op=ALU.add),
                 [Br, Bc], [Br])

            def reduce_sin(ang_, kf_, ki_, out_, bb, rows):
                P.op(DVE, lambda e: e.tensor_scalar(out=kf_, in0=ang_, scalar1=1.0 / TWO_PI, scalar2=None, op0=ALU.mult), [bb], [bb])
                P.op(DVE, lambda e: e.tensor_copy(ki_, kf_), [bb], [bb])
                P.op(DVE, lambda e: e.tensor_copy(kf_, ki_), [bb], [bb])
                P.op(DVE, lambda e: e.scalar_tensor_tensor(out=ang_, in0=kf_, scalar=-CW1, in1=ang_, op0=ALU.mult, op1=ALU.add), [bb], [bb])
                P.op(DVE, lambda e: e.scalar_tensor_tensor(out=ang_, in0=kf_, scalar=-CW2, in1=ang_, op0=ALU.mult, op1=ALU.add), [bb], [bb])
                P.op(DVE, lambda e: e.tensor_scalar(out=ang_, in0=ang_, scalar1=math.pi, scalar2=-math.pi, op0=ALU.min, op1=ALU.max), [bb], [bb])
                P.op(ACT, lambda e: e.activation(out=out_, in_=ang_, func=AF.Sin), [bb], [bb])

            reduce_sin(ang, kf, ki, sc, Br, 128)
            kiT = ar(T, I32)
            kfT = ar(T, F32)
            angc = ar(T, F32)
            angs = ar(T, F32)
            pTi = kiT
            pTf = kfT
            cos2 = angc
            sin2 = angs
            Bq = B["rope_f"]
            P.dma(SP, posT_i[:], pos_d[seq:seq + 1, t0:t0 + T].partition_broadcast(64), writes=[Bq])
            P.op(DVE, lambda e: e.tensor_copy(pTf[0:64, :], posT_i[:]), [Bq], [Bq])
            P.op(DVE, lambda e: e.tensor_scalar(out=angs[0:64, :], in0=pTf[0:64, :], scalar1=cst[0:64, C_INVF2:C_INVF2 + 1], scalar2=None,
                                                op0=ALU.mult), [Bq, Bc], [Bq])
            P.op(DVE, lambda e: e.tensor_scalar(out=angc[0:64, :], in0=angs[0:64, :], scalar1=math.pi / 2, scalar2=None, op0=ALU.add), [Bq], [Bq])
            reduce_sin(angs[0:64, :], kfT[0:64, :], kiT[0:64, :], sin2[0:64, :], Bq, 64)
            reduce_sin(angc[0:64, :], kfT[0:64, :], kiT[0:64, :], cos2[0:64, :], Bq, 64)
            P.op(DVE, lambda e: e.tensor_scalar(out=sin2[0:64, :], in0=sin2[0:64, :], scalar1=cst[0:64, C_SGN:C_SGN + 1], scalar2=None,
                                                op0=ALU.mult), [Bq, Bc], [Bq])
            return sc, cos2, sin2

        def attn_phase(seq, ti):
            ar_reset()
            t0 = ti * T
            hba = ar(NCH * D, BF16).rearrange("p (c d) -> p c d", c=NCH)
            sc, cos2, sin2 = rope_tables(seq, ti)
            Br, Bq = B["rope_t"], B["rope_f"]
            sc3 = sc.rearrange("p (c i) -> p c i", c=NCH)
            wk, bwk = wload("kvd")
            wk3 = wk.rearrange("p (k n) -> p k n", k=8)
            ckvT = ar(2 * T, BF16).rearrange("p (j t) -> p j t", j=2)
            cn = ar(256, BF16)
            ssk = ar(1, F32)
            junk = ar(384, F32)
            kr = ar(64, F32)
            t1 = ar(32, F32)
            t2 = ar(32, F32)
            krb = ar(64, BF16)
            for c in range(NCH):
                cs = slice(c * 128, (c + 1) * 128)
                pk = c % 2
                bp = B["ps%d" % pk]
                for kc in range(8):
                    P.op(PE, lambda e, cs=cs, kc=kc, pk=pk: e.matmul(PS[pk][:, 0:320], lhsT=hT[:, kc, cs], rhs=wk3[:, kc, :],
                                                                     start=(kc == 0), stop=(kc == 7)), [B["hT"], bwk], [bp])
                P.op(ACT, lambda e, pk=pk: e.activation(out=junk[:, 0:256], in_=PS[pk][:, 0:256], func=AF.Square, accum_out=ssk),
                     [bp], [B["ssk"], B["junk"]])
                rstd_from_ss(ssk, 256.0, RMS_EPS, [B["ssk"]])
                P.op(DVE, lambda e, pk=pk: e.tensor_scalar(out=cn, in0=PS[pk][:, 0:256], scalar1=ssk[:, 0:1], scalar2=None, op0=ALU.mult),
                     [bp, B["ssk"]], [B["cn"]])
                sn = sc3[:, c, 0:32]
                co = sc3[:, c, 32:64]
                x1 = PS[pk][:, 256:288]
                x2 = PS[pk][:, 288:320]
                Bk = B["kr"]
                P.op(DVE, lambda e, x1=x1, co=co: e.tensor_tensor(out=t1, in0=x1, in1=co, op=ALU.mult), [bp, Br], [Bk])
                P.op(DVE, lambda e, x2=x2, sn=sn: e.tensor_tensor(out=t2, in0=x2, in1=sn, op=ALU.mult), [bp, Br], [Bk])
                P.op(DVE, lambda e: e.tensor_tensor(out=kr[:, 0:32], in0=t1, in1=t2, op=ALU.subtract), [Bk], [Bk])
                P.op(DVE, lambda e, x2=x2, co=co: e.tensor_tensor(out=t1, in0=x2, in1=co, op=ALU.mult), [bp, Br, Bk], [Bk])
                P.op(DVE, lambda e, x1=x1, sn=sn: e.tensor_tensor(out=t2, in0=x1, in1=sn, op=ALU.mult), [bp, Br, Bk], [Bk])
                P.op(DVE, lambda e: e.tensor_tensor(out=kr[:, 32:64], in0=t1, in1=t2, op=ALU.add), [Bk], [Bk])
                P.op(DVE, lambda e: e.tensor_copy(krb, kr), [Bk], [B["krb"]])
                for j in range(2):
                    P.op(PE, lambda e, j=j: e.transpose(PSB[2][:, j * 128:(j + 1) * 128], cn[:, j * 128:(j + 1) * 128], identb[:]),
                         [B["cn"], B["identb"]], [B["ps2"]])
                    P.op(ACT, lambda e, j=j, cs=cs: e.activation(out=ckvT[:, j, cs], in_=PSB[2][:, j * 128:(j + 1) * 128], func=AF.Copy,
                                                                 scale=cst[:, C_KVG + j:C_KVG + j + 1]), [B["ps2"], Bc], [B["ckvT"]])
                P.op(PE, lambda e: e.transpose(PSB[3][0:64, 0:128], krb, identb[:]), [B["krb"], B["identb"]], [B["ps3"]])
                P.op(DVE, lambda e, c=c: e.tensor_copy(KrT[:, t0 + c * 128:t0 + (c + 1) * 128], PSB[3][0:64, 0:128]), [B["ps3"]], [B["KrT"]])
            wuk, bwuk = wload("kuk")
            wuk3 = wuk.rearrange("p (k n) -> p k n", k=2)
            for h in range(AH):
                pk = h % 2
                bp = B["ps%d" % pk]
                for j in range(2):
                    P.op(PE, lambda e, h=h, j=j, pk=pk: e.matmul(PS[pk][:, :], lhsT=wuk3[:, j, h * 128:(h + 1) * 128], rhs=ckvT[:, j, :],
                                                                 start=(j == 0), stop=(j == 1)), [B["ckvT"], bwuk], [bp])
                if h % 2 == 0:
                    P.op(ACT, lambda e, h=h, pk=pk: e.activation(out=KnT[:, h, t0:t0 + T], in_=PS[pk][:, :], func=AF.Copy), [bp], [B["KnT"]])
                else:
                    P.op(DVE, lambda e, h=h, pk=pk: e.tensor_copy(KnT[:, h, t0:t0 + T], PS[pk][:, :]), [bp], [B["KnT"]])
            wuv, bwuv = wload("kuv")
            wuv3 = wuv.rearrange("p (k n) -> p k n", k=2)
            for c in range(NCH):
                cs = slice(c * 128, (c + 1) * 128)
                for half in range(2):
                    pk = 2 + half
                    bp = B["ps%d" % pk]
                    for j in range(2):
                        P.op(PE, lambda e, cs=cs, j=j, half=half, pk=pk: e.matmul(PS[pk][:, :], lhsT=ckvT[:, j, cs],
                                                                                  rhs=wuv3[:, j, half * 512:(half + 1) * 512],
                                                                                  start=(j == 0), stop=(j == 1)), [B["ckvT"], bwuv], [bp])
                    blk = ti * NCH + c
                    if half == 0:
                        P.op(ACT, lambda e, blk=blk, pk=pk: e.activation(out=Vt[:, blk, 0:512], in_=PS[pk][:, :], func=AF.Copy), [bp], [B["Vt"]])
                    else:
                        P.op(DVE, lambda e, blk=blk, pk=pk: e.tensor_copy(Vt[:, blk, 512:1024], PS[pk][:, :]), [bp], [B["Vt"]])
            wq, bwq = wload("qd")
            wq3 = wq.rearrange("p (k n) -> p k n", k=8)
            cqT = ar(3 * T, BF16).rearrange("p (j t) -> p j t", j=3)
            cqn = ar(384, BF16)
            ssq = ar(1, F32)
            for c in range(NCH):
                cs = slice(c * 128, (c + 1) * 128)
                pk = c % 2
                bp = B["ps%d" % pk]
                for kc in range(8):
                    P.op(PE, lambda e, cs=cs, kc=kc, pk=pk: e.matmul(PS[pk][:, 0:384], lhsT=hT[:, kc, cs], rhs=wq3[:, kc, :],
                                                                     start=(kc == 0), stop=(kc == 7)), [B["hT"], bwq], [bp])
                P.op(ACT, lambda e, pk=pk: e.activation(out=junk[:, 0:384], in_=PS[pk][:, 0:384], func=AF.Square, accum_out=ssq),
                     [bp], [B["ssq"], B["junk"]])
                rstd_from_ss(ssq, 384.0, RMS_EPS, [B["ssq"]])
                P.op(DVE, lambda e, pk=pk: e.tensor_scalar(out=cqn, in0=PS[pk][:, 0:384], scalar1=ssq[:, 0:1], scalar2=None, op0=ALU.mult),
                     [bp, B["ssq"]], [B["cqn"]])
                for j in range(3):
                    P.op(PE, lambda e, j=j: e.transpose(PSB[2][:, j * 128:(j + 1) * 128], cqn[:, j * 128:(j + 1) * 128], identb[:]),
                         [B["cqn"], B["identb"]], [B["ps2"]])
                    P.op(ACT, lambda e, j=j, cs=cs: e.activation(out=cqT[:, j, cs], in_=PSB[2][:, j * 128:(j + 1) * 128], func=AF.Copy,
                                                                 scale=cst[:, C_QG + j:C_QG + j + 1]), [B["ps2"], Bc], [B["cqT"]])
            QnT = ar(AH * T, BF16).rearrange("p (h t) -> p h t", h=AH)
            QrT = ar(AH * T, BF16).rearrange("p (h t) -> p h t", h=AH)
            ra = ar(T, F32)
            rb = ar(T, F32)
            wn, bwn = wload("qun")
            wn3 = wn.rearrange("p (k n) -> p k n", k=3)
            for h in range(AH):
                pk = h % 2
                bp = B["ps%d" % pk]
                for j in range(3):
                    P.op(PE, lambda e, h=h, j=j, pk=pk: e.matmul(PS[pk][:, :], lhsT=wn3[:, j, h * 128:(h + 1) * 128], rhs=cqT[:, j, :],
                                                                 start=(j == 0), stop=(j == 2)), [B["cqT"], bwn], [bp])
                P.op(ACT, lambda e, h=h, pk=pk: e.activation(out=QnT[:, h, :], in_=PS[pk][:, :], func=AF.Copy, scale=SCALE), [bp], [B["QnT"]])
            wr, bwr = wload("qur")
            wr3 = wr.rearrange("p (k n) -> p k n", k=3)
            wsw, bws = wload("qus")
            ws3 = wsw.rearrange("p (k n) -> p k n", k=3)
            for h in range(AH):
                for j in range(3):
                    P.op(PE, lambda e, h=h, j=j: e.matmul(PS[2][0:64, :], lhsT=wr3[:, j, h * 64:(h + 1) * 64], rhs=cqT[:, j, :],
                                                          start=(j == 0), stop=(j == 2)), [B["cqT"], bwr], [B["ps2"]])
                for j in range(3):
                    P.op(PE, lambda e, h=h, j=j: e.matmul(PS[3][0:64, :], lhsT=ws3[:, j, h * 64:(h + 1) * 64], rhs=cqT[:, j, :],
                                                          start=(j == 0), stop=(j == 2)), [B["cqT"], bws], [B["ps3"]])
                Bra = B["ra"]
                P.op(DVE, lambda e: e.tensor_tensor(out=ra[0:64, :], in0=PS[2][0:64, :], in1=cos2[0:64, :], op=ALU.mult), [B["ps2"], Bq], [Bra])
                P.op(DVE, lambda e: e.tensor_tensor(out=rb[0:64, :], in0=PS[3][0:64, :], in1=sin2[0:64, :], op=ALU.mult), [B["ps3"], Bq], [B["rb"]])
                P.op(DVE, lambda e: e.tensor_tensor(out=ra[0:64, :], in0=ra[0:64, :], in1=rb[0:64, :], op=ALU.add), [Bra, B["rb"]], [Bra])
                P.op(ACT, lambda e, h=h: e.activation(out=QrT[0:64, h, :], in_=ra[0:64, :], func=AF.Copy, scale=SCALE), [Bra], [B["QrT"]])
            OTs = QnT
            PT = [ar(T, BF16) for _ in range(3)]
            rl = ar(T, F32)
            nkb = (ti + 1) * NCH
            it = 0
            for h in range(AH):
                for j in range(nkb):
                    jj = j - ti * NCH
                    lo = 0 if jj < 0 else jj * 128
                    ks = slice(j * 128, (j + 1) * 128)
                    ps_ = it % 2
                    pt_ = PT[it % 3]
                    bps, bpt = B["ps%d" % ps_], B["PT%d" % (it % 3)]
                    it += 1
                    diag = jj >= 0
                    P.op(PE, lambda e, h=h, ks=ks, lo=lo, ps_=ps_: e.matmul(PS[ps_][:, lo:T], lhsT=KnT[:, h, ks], rhs=QnT[:, h, lo:T],
                                                                          start=True, stop=False), [B["KnT"], B["QnT"]], [bps])
                    P.op(PE, lambda e, h=h, ks=ks, lo=lo, ps_=ps_, diag=diag: e.matmul(PS[ps_][:, lo:T], lhsT=KrT[0:64, ks], rhs=QrT[0:64, h, lo:T],
                                                                                     start=False, stop=(not diag)), [B["KrT"], B["QrT"]], [bps])
                    if diag:
                        P.op(PE, lambda e, lo=lo, ps_=ps_: e.matmul(PS[ps_][:, lo:lo + 128], lhsT=identb[:], rhs=negmb[:], start=False, stop=True),
                             [B["identb"], B["negmb"]], [bps])
                    P.op(ACT, lambda e, lo=lo, ps_=ps_, pt_=pt_: e.activation(out=pt_[:, lo:T], in_=PS[ps_][:, lo:T], func=AF.Exp), [bps], [bpt])
                    P.op(PE, lambda e, h=h, j=j, lo=lo, pt_=pt_: e.matmul(PS[2][:, lo:T], lhsT=Vt[:, j, h * 128:(h + 1) * 128], rhs=pt_[:, lo:T],
                                                                        start=(j == 0), stop=(j == nkb - 1)), [B["Vt"], bpt], [B["ps2"]])
                    P.op(PE, lambda e, lo=lo, pt_=pt_, j=j: e.matmul(PS[3][:, lo:T], lhsT=onesb[:], rhs=pt_[:, lo:T],
                                                                   start=(j == 0), stop=(j == nkb - 1)), [B["onesb"], bpt], [B["ps3"]])
                P.op(DVE, lambda e: e.reciprocal(out=rl, in_=PS[3][:, :]), [B["ps3"]], [B["rl"]])
                P.op(DVE, lambda e, h=h: e.tensor_tensor(out=OTs[:, h, :], in0=PS[2][:, :], in1=rl, op=ALU.mult), [B["ps2"], B["rl"]], [B["QnT"]])
            for half in range(2):
                wo, bwo = wload("ao%d" % half)
                wo3 = wo.rearrange("p (k n) -> p k n", k=8)
                for c in range(NCH):
                    for h in range(AH):
                        P.op(PE, lambda e, c=c, h=h, wo3=wo3: e.matmul(PS[4 + c][:, :], lhsT=OTs[:, h, c * 128:(c + 1) * 128], rhs=wo3[:, h, :],
                                                                       start=(h == 0), stop=(h == AH - 1)), [B["QnT"], bwo], [B["ps%d" % (4 + c)]])
                for c in range(NCH):
                    hs = hres[:, c, half * 512:(half + 1) * 512]
                    P.op(DVE, lambda e, c=c, hs=hs: e.scalar_tensor_tensor(out=hs, in0=hs, scalar=ALPHA, in1=PS[4 + c][:, :], op0=ALU.mult, op1=ALU.add),
                         [B["hres%d" % c], B["ps%d" % (4 + c)]], [B["hres%d" % c]])
            layer_norm_tile(2, hba)

        stop_after = (debug or {}).get("_stop", None)

        def dump(name, t0):
            if debug and name in dbg_d:
                for c in range(NCH):
                    P.dma(SP, dbg_d[name][t0 + c * 128:t0 + (c + 1) * 128, :], hres[:, c, :], reads=[B["hres%d" % c]], writes=[B["dbgo"]])

        def main_loop():
            for seq in range(nseq):
                P.op(POOL, lambda e: e.memset(stf[:], 0.0), [], [B["stf%d" % g] for g in range(NG)])
                P.op(POOL, lambda e: e.memset(stb[:], 0.0), [], [B["stb%d" % g] for g in range(NG)])
                P.op(POOL, lambda e: e.memset(halo_s[:], 0.0), [], [B["halo_s"]])
                for l in range(2):
                    P.op(POOL, lambda e, l=l: e.memset(halo_f[l][:], 0.0), [], [B["halo_f%d" % l]])
                for ti in range(NT):
                    t0 = ti * T
                    ssd_phase(seq, ti)
                    dump("h1", t0)
                    P.barrier()
                    if stop_after == "ssd":
                        return
                    ffn_phase(0)
                    dump("h2", t0)
                    P.barrier()
                    if stop_after == "ffn0":
                        return
                    attn_phase(seq, ti)
                    dump("h3", t0)
                    P.barrier()
                    if stop_after == "attn":
                        return
                    ffn_phase(1)
                    for c in range(NCH):
                        P.dma(SP, out_d[seq, t0 + c * 128:t0 + (c + 1) * 128, :], hres[:, c, :], reads=[B["hres%d" % c]], writes=[B["outd"]])
                    P.barrier()

        if stop_after != "pro":
            main_loop()
        P.op(POOL, lambda e: e.memset(fin[:], 0.0), [B["outd"], B["dbgo"]], [B["fin"]])
        P.emit()
        build_program.stats = (P.stats, P.nsem)
    return nc


_INPUT_KEYS = None


def _prep(inputs):
    inp = {k: np.asarray(v) for k, v in inputs.items()}
    slabs = _slab_defs(inp)
    wall = np.empty((128, TOT), np.float32)
    for k, a in slabs.items():
        assert a.shape[1] == SLAB_N[k], (k, a.shape, SLAB_N[k])
        wall[:, SLAB_OFF[k]:SLAB_OFF[k] + SLAB_N[k]] = a
    cst = _pack_consts(inp)
    lnp = np.stack([inp["ln_mix_g"][0], inp["ln_mix_b"][0], inp["ln_ffn_g"][0], inp["ln_ffn_b"][0],
                    inp["ln_mix_g"][1], inp["ln_mix_b"][1], inp["ln_ffn_g"][1], inp["ln_ffn_b"][1]]).astype(np.float32)
    return wall, cst, lnp


def kernel(**inputs):
    x = np.asarray(inputs["x"], dtype=np.float32)
    pos = np.asarray(inputs["positions"]).astype(np.int32)
    Bt, S, _ = x.shape
    ncores = 8
    nseq = Bt // ncores
    wall, cst, lnp = _prep(inputs)
    nc = build_program(nseq, S)
    in_maps = []
    for i in range(ncores):
        in_maps.append({
            "x": np.ascontiguousarray(x[i * nseq:(i + 1) * nseq]),
            "pos": np.ascontiguousarray(pos[i * nseq:(i + 1) * nseq]),
            "wall": wall, "cst": cst, "lnp": lnp,
        })
    res = run_bass_kernel_spmd(nc, in_maps, core_ids=list(range(ncores)))
    out = np.concatenate([np.asarray(r["out"], dtype=np.float32) for r in res.results], axis=0)
    return out
```
